# Optimizing a Trainium2 kernel written in Bass

```python
import math
import jax, jax.numpy as jnp
from jax import lax
import numpy as np

D_MODEL = 2048
BATCH = 4
SEQ = 8192
DEPTH = 4

MIX_WIDTH = D_MODEL
DN_HEADS = 8
DN_HEAD_DIM = 128
DN_WIDTH = DN_HEADS * DN_HEAD_DIM
DN_CHUNK = 64
SHORT_CONV = 4
POOL_WINDOWS = (2, 4, 8, 16)
POOL_GROUPS = len(POOL_WINDOWS)
POOL_WIDTH = MIX_WIDTH - DN_WIDTH
POOL_GROUP_DIM = POOL_WIDTH // POOL_GROUPS
EVEN_IN = 4 * DN_WIDTH + 2 * DN_HEADS + POOL_WIDTH
CONF_WIDTH = D_MODEL
CONF_WIN = 31
D_FF = 4 * D_MODEL
N_EVEN = (DEPTH + 1) // 2
N_ODD = DEPTH // 2
EPS = 1e-6

kernel_name = 'hybrid_deltanet_pool_conformer_trunk'


def rms_norm(x, g):
    xf = x.astype(jnp.float32)
    y = xf * lax.rsqrt(jnp.mean(xf * xf, axis=-1, keepdims=True) + EPS)
    return (y * g.astype(jnp.float32)).astype(x.dtype)


def layer_norm(x, g, b):
    xf = x.astype(jnp.float32)
    mu = jnp.mean(xf, axis=-1, keepdims=True)
    xc = xf - mu
    y = xc * lax.rsqrt(jnp.mean(xc * xc, axis=-1, keepdims=True) + EPS)
    return (y * g.astype(jnp.float32) + b.astype(jnp.float32)).astype(x.dtype)


def l2_normalize(x):
    return x * lax.rsqrt(jnp.sum(x * x, axis=-1, keepdims=True) + EPS)


def causal_depthwise_conv(x, w):
    K, C = w.shape
    return lax.conv_general_dilated(
        x, w[:, None, :].astype(x.dtype), window_strides=(1,), padding=[(K - 1, 0)],
        dimension_numbers=('NWC', 'WIO', 'NWC'), feature_group_count=C)


def gated_delta_rule(q, k, v, g, beta):
    B_, S_, H, Dk = q.shape
    Dv = v.shape[-1]
    C = DN_CHUNK
    N = S_ // C

    def to_chunks(t):
        t = t.reshape((B_, N, C, H) + t.shape[3:])
        return jnp.moveaxis(t, 3, 1)

    q, k, v, g, beta = map(to_chunks, (q * Dk ** -0.5, k, v, g, beta))
    gc = jnp.cumsum(g, axis=-1)
    pos = jnp.arange(C)
    causal = pos[:, None] >= pos[None, :]
    strict = pos[:, None] > pos[None, :]
    gamma = jnp.exp(jnp.where(causal, gc[..., :, None] - gc[..., None, :], -jnp.inf))
    kb = k * beta[..., None]
    a_mat = jnp.where(strict, jnp.einsum('bhnck,bhnmk->bhncm', kb, k) * gamma, 0.0)
    eye = jnp.eye(C, dtype=jnp.float32)
    t_inv = lax.linalg.triangular_solve(a_mat + eye, jnp.broadcast_to(eye, a_mat.shape),
                                        left_side=True, lower=True, unit_diagonal=True)
    u = jnp.einsum('bhncm,bhnmv->bhncv', t_inv, v * beta[..., None])
    w = jnp.einsum('bhncm,bhnmk->bhnck', t_inv, kb * jnp.exp(gc)[..., None])
    attn = jnp.where(causal, jnp.einsum('bhnck,bhnmk->bhncm', q, k) * gamma, 0.0)
    qg = q * jnp.exp(gc)[..., None]
    g_last = gc[..., -1]
    kd = k * jnp.exp(g_last[..., None] - gc)[..., None]
    decay = jnp.exp(g_last)

    def step(state, inp):
        u_n, w_n, attn_n, qg_n, kd_n, decay_n = inp
        v_new = u_n - jnp.einsum('bhck,bhkv->bhcv', w_n, state)
        o_n = (jnp.einsum('bhck,bhkv->bhcv', qg_n, state)
               + jnp.einsum('bhcm,bhmv->bhcv', attn_n, v_new))
        state = state * decay_n[..., None, None] + jnp.einsum('bhck,bhcv->bhkv', kd_n, v_new)
        return state, o_n

    xs = tuple(jnp.moveaxis(t, 2, 0) for t in (u, w, attn, qg, kd, decay))
    state0 = jnp.zeros((B_, H, Dk, Dv), jnp.float32)
    _, o = lax.scan(step, state0, xs)
    return jnp.transpose(o, (1, 0, 3, 2, 4)).reshape(B_, S_, H, Dv)


def multiscale_pool(x, w_grp, scale):
    B_, S_, _ = x.shape
    xf = x.astype(jnp.float32)
    cs = jnp.cumsum(xf, axis=1)
    count = jnp.arange(1, S_ + 1, dtype=jnp.float32)[:, None]
    outs = []
    for gi, win in enumerate(POOL_WINDOWS):
        sl = slice(gi * POOL_GROUP_DIM, (gi + 1) * POOL_GROUP_DIM)
        c = cs[..., sl]
        lower = jnp.pad(c, ((0, 0), (win, 0), (0, 0)))[:, :S_]
        outs.append((c - lower) / jnp.minimum(count, win) - xf[..., sl])
    pooled = jnp.concatenate(outs, axis=-1).astype(x.dtype).reshape(B_, S_, POOL_GROUPS, POOL_GROUP_DIM)
    y = jnp.einsum('bsgc,gcd->bsgd', pooled, w_grp).reshape(B_, S_, POOL_WIDTH)
    return y * scale


def even_mixer(h, w_in, conv_w, a_log, dt_bias, dn_norm, pool_w, pool_scale, w_out):
    B_, S_, _ = h.shape
    p = h @ w_in
    o1 = 3 * DN_WIDTH
    o2 = o1 + DN_WIDTH
    o3 = o2 + DN_HEADS
    o4 = o3 + DN_HEADS
    qkv = jax.nn.silu(causal_depthwise_conv(p[..., :o1], conv_w))
    z = p[..., o1:o2].reshape(B_, S_, DN_HEADS, DN_HEAD_DIM)
    b_logit = p[..., o2:o3].astype(jnp.float32)
    a_logit = p[..., o3:o4].astype(jnp.float32)
    xp = p[..., o4:]
    q, k, v = [t.reshape(B_, S_, DN_HEADS, DN_HEAD_DIM).astype(jnp.float32)
               for t in jnp.split(qkv, 3, axis=-1)]
    q = l2_normalize(q)
    k = l2_normalize(k)
    beta = jax.nn.sigmoid(b_logit)
    g = -jnp.exp(a_log.astype(jnp.float32)) * jax.nn.softplus(a_logit + dt_bias.astype(jnp.float32))
    o = gated_delta_rule(q, k, v, g, beta).astype(h.dtype)
    o = (rms_norm(o, dn_norm) * jax.nn.silu(z)).reshape(B_, S_, DN_WIDTH)
    y_pool = multiscale_pool(xp, pool_w, pool_scale)
    return jnp.concatenate([o, y_pool], axis=-1) @ w_out


def odd_mixer(h, w_in, dw_w, dw_b, ln_g, ln_b, w_out):
    a, gate = jnp.split(h @ w_in, 2, axis=-1)
    u = a * jax.nn.sigmoid(gate)
    u = causal_depthwise_conv(u, dw_w) + dw_b
    u = jax.nn.silu(layer_norm(u, ln_g, ln_b))
    return u @ w_out


def squared_relu_mlp(h, w_up, w_down):
    return jnp.square(jax.nn.relu(h @ w_up)) @ w_down


def setup_inputs(seed: int = 0) -> dict:
    key = jax.random.key(seed)
    ks = jax.random.split(key, 24)
    f32 = jnp.float32

    def nrm(k, shape, fan_in):
        return jax.random.normal(k, shape, f32) * fan_in ** -0.5

    def gain(k, shape):
        return 1.0 + 0.05 * jax.random.normal(k, shape, f32)

    dt = jnp.exp(jax.random.uniform(ks[9], (N_EVEN, DN_HEADS), f32, math.log(1e-3), math.log(1e-1)))
    return {
        'x': jax.random.normal(ks[0], (BATCH, SEQ, D_MODEL), f32),
        'norm_mix_pre': gain(ks[1], (DEPTH, D_MODEL)),
        'norm_mix_post': gain(ks[2], (DEPTH, D_MODEL)),
        'norm_mlp_pre': gain(ks[3], (DEPTH, D_MODEL)),
        'norm_mlp_post': gain(ks[4], (DEPTH, D_MODEL)),
        'even_w_in': nrm(ks[5], (N_EVEN, D_MODEL, EVEN_IN), D_MODEL),
        'even_conv': nrm(ks[6], (N_EVEN, SHORT_CONV, 3 * DN_WIDTH), SHORT_CONV),
        'even_a_log': jnp.log(jax.random.uniform(ks[8], (N_EVEN, DN_HEADS), f32, 1.0, 16.0)),
        'even_dt_bias': dt + jnp.log(-jnp.expm1(-dt)),
        'even_dn_norm': gain(ks[10], (N_EVEN, DN_HEAD_DIM)),
        'even_pool_w': nrm(ks[11], (N_EVEN, POOL_GROUPS, POOL_GROUP_DIM, POOL_GROUP_DIM), POOL_GROUP_DIM),
        'even_pool_scale': gain(ks[12], (N_EVEN, POOL_WIDTH)),
        'even_w_out': nrm(ks[13], (N_EVEN, MIX_WIDTH, D_MODEL), MIX_WIDTH),
        'odd_w_in': nrm(ks[14], (N_ODD, D_MODEL, 2 * CONF_WIDTH), D_MODEL),
        'odd_dw': nrm(ks[15], (N_ODD, CONF_WIN, CONF_WIDTH), CONF_WIN),
        'odd_dw_b': 0.02 * jax.random.normal(ks[16], (N_ODD, CONF_WIDTH), f32),
        'odd_ln_g': gain(ks[17], (N_ODD, CONF_WIDTH)),
        'odd_ln_b': 0.02 * jax.random.normal(ks[18], (N_ODD, CONF_WIDTH), f32),
        'odd_w_out': nrm(ks[19], (N_ODD, CONF_WIDTH, D_MODEL), CONF_WIDTH),
        'mlp_w_up': nrm(ks[20], (DEPTH, D_MODEL, D_FF), D_MODEL),
        'mlp_w_down': nrm(ks[21], (DEPTH, D_FF, D_MODEL), D_FF),
    }


def reference(x, norm_mix_pre, norm_mix_post, norm_mlp_pre, norm_mlp_post,
              even_w_in, even_conv, even_a_log, even_dt_bias, even_dn_norm,
              even_pool_w, even_pool_scale, even_w_out,
              odd_w_in, odd_dw, odd_dw_b, odd_ln_g, odd_ln_b, odd_w_out,
              mlp_w_up, mlp_w_down):
    for i in range(DEPTH):
        j = i // 2
        h = rms_norm(x, norm_mix_pre[i])
        if i % 2 == 0:
            mix = even_mixer(h, even_w_in[j], even_conv[j], even_a_log[j], even_dt_bias[j],
                             even_dn_norm[j], even_pool_w[j], even_pool_scale[j], even_w_out[j])
        else:
            mix = odd_mixer(h, odd_w_in[j], odd_dw[j], odd_dw_b[j], odd_ln_g[j], odd_ln_b[j], odd_w_out[j])
        x = x + rms_norm(mix, norm_mix_post[i])
        ff = squared_relu_mlp(rms_norm(x, norm_mlp_pre[i]), mlp_w_up[i], mlp_w_down[i])
        x = x + rms_norm(ff, norm_mlp_post[i])
    return x
```

```python
import os
import numpy as np
from contextlib import ExitStack
import concourse.bass as bass
import concourse.mybir as mybir
from concourse.bass_utils import run_bass_kernel_spmd

F32 = mybir.dt.float32
BF16 = mybir.dt.bfloat16
AF = mybir.ActivationFunctionType
ALU = mybir.AluOpType
AX = mybir.AxisListType

D = 2048
KC = 16
DFF = 8192
FC = 64
EVEN_IN = 5136
NH = 8
EPS = 1e-6
TT_ = 512
NB = 4
CH = 64
NCH = TT_ // CH
NEG = -30000.0
NVEC = 608
N_CORES = 4
T_CORE = 8192


class Eng:
    def __init__(self, kb, name, h, same_wait):
        self.kb, self.name, self.h, self.same_wait = kb, name, h, same_wait
        self.waited = {}
        self.pend_r, self.pend_w = [], []
        self.new_sem()

    def new_sem(self):
        self.sem = self.kb.new_sem(self.name)
        self.count = 0

    def wait(self, tok):
        if tok is None:
            return
        sem, val, src = tok
        if src is self and not self.same_wait:
            return
        key = id(sem)
        if self.waited.get(key, 0) >= val:
            return
        self.waited[key] = val
        self.h.wait_ge(sem, val)


class KB:
    def __init__(self, nc, stack):
        self.nc, self.stack = nc, stack
        self.n_sem = 0
        self.n_inst = 0
        self.res_w, self.res_r, self.dsems = {}, {}, {}
        self.pe = Eng(self, "pe", nc.tensor, False)
        self.act = Eng(self, "act", nc.scalar, True)
        self.dve = Eng(self, "dve", nc.vector, True)
        self.pool = Eng(self, "pool", nc.gpsimd, True)
        self.sp = Eng(self, "sp", nc.sync, False)
        self.engs = [self.pe, self.act, self.dve, self.pool, self.sp]

    def new_sem(self, name):
        self.n_sem += 1
        return self.stack.enter_context(self.nc.semaphore(f"s{self.n_sem}_{name}"))

    def _deps(self, reads, writes):
        deps = []
        for r in reads:
            t = self.res_w.get(r)
            if t is not None:
                deps.append(t)
        for w in writes:
            t = self.res_w.get(w)
            if t is not None:
                deps.append(t)
            deps.extend(self.res_r.get(w, ()))
        return deps

    def _commit(self, tok, reads, writes):
        for r in reads:
            self.res_r.setdefault(r, []).append(tok)
        for w in writes:
            self.res_w[w] = tok
            self.res_r[w] = []

    def op(self, eng, fn, reads=(), writes=(), sig=True):
        for t in self._deps(reads, writes):
            eng.wait(t)
        inst = fn(eng.h)
        self.n_inst += 1
        if sig:
            eng.count += 1
            inst.then_inc(eng.sem, 1)
            tok = (eng.sem, eng.count, eng)
            self._commit(tok, list(reads) + eng.pend_r, list(writes) + eng.pend_w)
            eng.pend_r, eng.pend_w = [], []
            return tok
        eng.pend_r += list(reads)
        eng.pend_w += list(writes)
        return None

    def dma(self, eng, out, in_, reads=(), writes=(), semkey=None, **kw):
        for t in self._deps(reads, writes):
            eng.wait(t)
        if semkey is None:
            semkey = writes[0] if writes else reads[0]
        ent = self.dsems.get(semkey)
        if ent is None:
            ent = self.dsems[semkey] = [self.new_sem("d"), 0]
        inst = eng.h.dma_start(out=out, in_=in_, **kw)
        ent[1] += 16
        inst.then_inc(ent[0], 16)
        self.n_inst += 1
        tok = (ent[0], ent[1], None)
        self._commit(tok, reads, writes)
        return tok

    def barrier(self, engs=None):
        toks = []
        for e in self.engs:
            if e.count:
                toks.append((e.sem, e.count, None))
        for sem, cum in self.dsems.values():
            if cum:
                toks.append((sem, cum, None))
        for e in (engs or self.engs):
            for t in toks:
                e.wait(t)


def build_program(t_core=T_CORE, layers=(0, 1, 2, 3)):
    nc = bass.Bass("TRN2", target_bir_lowering=False)
    NT = t_core // TT_

    def din(name, shape):
        return nc.dram_tensor(name, list(shape), F32, kind="ExternalInput").ap()

    x_in = din("x", [t_core, D])
    vecs = din("vecs", [4, 128, NVEC])
    bcs = din("bcs", [2, 128, 144])
    c_ident = din("c_ident", [128, 128])
    c_masks = din("c_masks", [64, 256])
    c_invcnt = din("c_invcnt", [128, 2048])
    even_w_in = din("even_w_in", [2, D, EVEN_IN])
    even_pool_w = din("even_pool_w", [2, 1024, 256])
    even_w_out = din("even_w_out", [2, D, D])
    odd_w_in = din("odd_w_in", [2, D, 2 * D])
    odd_w_out = din("odd_w_out", [2, D, D])
    mlp_w_up = din("mlp_w_up", [4, D, DFF])
    mlp_w_down = din("mlp_w_down", [4, DFF, D])
    y_out = nc.dram_tensor("y", [t_core, D], F32, kind="ExternalOutput").ap()

    def dint(name, shape, dt=BF16):
        return nc.dram_tensor(name, list(shape), dt, kind="Internal").ap()

    xr = dint("xr", [t_core, D], F32)
    b_ewin = [dint(f"b_ewin{j}", [D, EVEN_IN]) for j in range(2)]
    b_epw = [dint(f"b_epw{j}", [1024, 256]) for j in range(2)]
    b_ewout = [dint(f"b_ewout{j}", [D, D]) for j in range(2)]
    b_owin = [dint(f"b_owin{j}", [D, 2 * D]) for j in range(2)]
    b_owout = [dint(f"b_owout{j}", [D, D]) for j in range(2)]
    b_wup = [dint(f"b_wup{i}", [D, DFF]) for i in range(4)]
    b_wdn = [dint(f"b_wdn{i}", [16, 128, FC * 128]) for i in range(4)]

    with ExitStack() as st:
        kb = KB(nc, st)
        pe, act, dve, pool, sp = kb.pe, kb.act, kb.dve, kb.pool, kb.sp

        uid = [0]

        def sbt(stack, name, shape, dt):
            uid[0] += 1
            return stack.enter_context(nc.sbuf_tensor(f"{name}_{uid[0]}", list(shape), dt))

        def pst(name, shape, dt):
            return st.enter_context(nc.psum_tensor(name, list(shape), dt))

        def conv_rows(dst, src, key, rows, step):
            for r in range(0, rows, step):
                kb.dma(pool, dst[r:r + step, :], src[r:r + step, :], writes=[key])

        def conv_wdn(i):
            for nb in range(16):
                for k0 in range(0, FC, 16):
                    src = mlp_w_down[i][k0 * 128:(k0 + 16) * 128, nb * 128:(nb + 1) * 128].rearrange(
                        "(kc p) c -> p kc c", p=128)
                    dst = b_wdn[i][nb][:, k0 * 128:(k0 + 16) * 128].rearrange("p (kc c) -> p kc c", c=128)
                    kb.dma(pool, dst, src, writes=[("w", "wdn", i)])

        for li in layers:
            j = li // 2
            if li % 2 == 0:
                conv_rows(b_ewin[j], even_w_in[j], ("w", "ewin", j), D, 128)
                conv_rows(b_epw[j], even_pool_w[j], ("w", "epw", j), 1024, 1024)
                conv_rows(b_ewout[j], even_w_out[j], ("w", "ewout", j), D, 256)
            else:
                conv_rows(b_owin[j], odd_w_in[j], ("w", "owin", j), D, 128)
                conv_rows(b_owout[j], odd_w_out[j], ("w", "owout", j), D, 256)
            conv_rows(b_wup[li], mlp_w_up[li], ("w", "wup", li), D, 128)
            conv_wdn(li)

        identf = sbt(st, "identf", [128, 128], F32)
        identb = sbt(st, "identb", [128, 128], BF16)
        onesf = sbt(st, "onesf", [128, 128], F32)
        masks = sbt(st, "masks", [64, 4, 64], F32)
        vec = sbt(st, "vec", [128, NVEC], F32)
        xt = [sbt(st, f"xt{i}", [128, D], F32) for i in range(2)]
        hb = sbt(st, "hb", [128, D], BF16)
        junk = sbt(st, "junk", [128, D], F32)
        hT = sbt(st, "hT", [128, KC, TT_], BF16)
        big = sbt(st, "big", [128, KC, TT_], F32)
        sq = [sbt(st, f"sq{i}", [128, TT_], F32) for i in range(2)]
        rb = sbt(st, "rb", [128, TT_], F32)
        ws = [sbt(st, f"ws{i}", [128, 8192], BF16) for i in range(2)]
        ssq = sbt(st, "ssq", [128, 1], F32)
        rstd = sbt(st, "rstd", [128, 1], F32)

        PB = [pst(f"PB{i}", [128, 1024], BF16) for i in range(2)]
        PF = [pst(f"PF{i}", [128, 512], F32) for i in range(4)]
        PW = pst("PW", [128, 1024], F32)

        DBG = {}
        if os.environ.get("KDEBUG_DUMP"):
            dbg_t = nc.dram_tensor("dbg", [128, 16384], F32, kind="ExternalOutput").ap()
            dscr = sbt(st, "dscr", [128, 1024], F32)
            dbg_off = [0]

            def dump(name, ap, key):
                shp = list(ap.shape)
                P = shp[0]
                n = int(np.prod(shp[1:]))
                if len(shp) == 2:
                    o = dscr[0:P, 0:n]
                elif len(shp) == 3:
                    o = dscr[0:P, 0:n].rearrange("p (a b) -> p a b", b=shp[2])
                kb.op(dve, lambda e: e.tensor_copy(out=o, in_=ap), reads=[key], writes=["dscr"])
                kb.dma(sp, dbg_t[0:P, dbg_off[0]:dbg_off[0] + n], dscr[0:P, 0:n], reads=["dscr"], writes=["dbg"])
                DBG[name] = (dbg_off[0], shp)
                dbg_off[0] += n
            build_program.DBG = DBG
        else:
            dump = None
        kb.dma(sp, identf[:], c_ident, writes=["identf"])
        kb.dma(sp, masks[:], c_masks.rearrange("p (a c) -> p a c", a=4), writes=["masks"])
        kb.op(dve, lambda e: e.tensor_copy(out=identb[:], in_=identf[:]), reads=["identf"], writes=["identb"])
        kb.op(dve, lambda e: e.memset(onesf[:], 1.0), writes=["onesf"])
        Um, Im, MC, MS = (masks[:, i, :] for i in range(4))

        ws_ctr = [0]

        def load_slab(src_ap, shape3, wkey):
            s = ws_ctr[0] % 2
            ws_ctr[0] += 1
            a, b = shape3
            view = ws[s][:, 0:a * b].rearrange("p (a b) -> p a b", b=b)
            kb.dma(sp, view, src_ap, reads=[wkey], writes=[("ws", s)])
            return view, ("ws", s)

        pf_ctr = [0]

        def next_pf():
            i = pf_ctr[0] % 2
            pf_ctr[0] += 1
            return PF[i], ("PF", i)

        def proj(wb, col0, nchunks, rhs, rhs_key, wkey, consumer, kchunks=KC, group=4):
            for g0 in range(0, nchunks, group):
                gn = min(group, nchunks - g0)
                src = wb[:, col0 + g0 * 128: col0 + (g0 + gn) * 128].rearrange("(kc p) n -> p kc n", p=128)
                view, skey = load_slab(src, (kchunks, gn * 128), wkey)
                for c in range(gn):
                    ps, pk = next_pf()
                    for kc in range(kchunks):
                        kb.op(pe, lambda e: e.matmul(ps[:], lhsT=view[:, kc, c * 128:(c + 1) * 128], rhs=rhs[:, kc, :],
                                                     start=(kc == 0), stop=(kc == kchunks - 1)),
                              reads=[skey, rhs_key], writes=[pk], sig=(kc == kchunks - 1))
                    consumer(g0 + c, ps, pk)

        def rsqrt_small(dst, src, scale, n_key_r, n_key_w):
            kb.op(act, lambda e: e.activation(out=dst, in_=src, func=AF.Sqrt, scale=scale, bias=EPS),
                  reads=n_key_r, writes=n_key_w)
            kb.op(dve, lambda e: e.reciprocal(out=dst, in_=dst), reads=n_key_w, writes=n_key_w)

        def prenorm_transpose(src, skey_fn, gcol):
            for b in range(NB):
                xs = xt[b % 2]
                xk = ("xt", b % 2)
                kb.dma(sp, xs[:], src[b * 128:(b + 1) * 128, :], reads=[skey_fn(b)], writes=[xk])
                kb.op(act, lambda e: e.activation(out=junk[:], in_=xs[:], func=AF.Square, accum_out=ssq[:]),
                      reads=[xk], writes=["junk", "ssq"])
                rsqrt_small(rstd[:], ssq[:], 1.0 / D, ["ssq"], ["rstd"])
                kb.op(dve, lambda e: e.tensor_scalar_mul(out=hb[:], in0=xs[:], scalar1=rstd[:, 0:1]),
                      reads=[xk, "rstd"], writes=["hb"])
                for half in range(2):
                    pb = PB[half]
                    for c in range(8):
                        kc = half * 8 + c
                        kb.op(pe, lambda e: e.transpose(out=pb[:, c * 128:(c + 1) * 128],
                                                        in_=hb[:, kc * 128:(kc + 1) * 128], identity=identb[:]),
                              reads=["hb", "identb"], writes=[("PB", half)], sig=(c == 7))
                    gsl = vec[:, gcol + half * 8: gcol + half * 8 + 8].unsqueeze(2).to_broadcast([128, 8, 128])
                    kb.op(dve, lambda e: e.tensor_tensor(out=hT[:, half * 8:(half + 1) * 8, b * 128:(b + 1) * 128],
                                                         in0=pb[:].rearrange("p (c t) -> p c t", c=8), in1=gsl,
                                                         op=ALU.mult),
                          reads=[("PB", half), "vec"], writes=["hT"])

        def postnorm_residual(gcol, src, skey_fn, dst, dkey_fn):
            for n in range(KC):
                s = sq[n % 2]
                kb.op(act, lambda e: e.activation(out=s[:], in_=big[:, n, :], func=AF.Square),
                      reads=["big"], writes=[("sq", n % 2)])
                kb.op(pe, lambda e: e.matmul(PF[2][:], lhsT=onesf[:], rhs=s[:], start=(n == 0), stop=(n == KC - 1)),
                      reads=[("sq", n % 2), "onesf"], writes=[("PF", 2)], sig=True)
            rsqrt_small(rb[:], PF[2][:], 1.0 / D, [("PF", 2)], ["rb"])
            for n in range(KC):
                kb.op(dve, lambda e: e.scalar_tensor_tensor(out=big[:, n, :], in0=big[:, n, :],
                                                            scalar=vec[:, gcol + n: gcol + n + 1], in1=rb[:],
                                                            op0=ALU.mult, op1=ALU.mult),
                      reads=["big", "rb", "vec"], writes=["big"])
            for b in range(NB):
                xs = xt[b % 2]
                xk = ("xt", b % 2)
                kb.dma(sp, xs[:], src[b * 128:(b + 1) * 128, :], reads=[skey_fn(b)], writes=[xk])
                for q in range(4):
                    pf, pk = PF[2 + q % 2], ("PF", 2 + q % 2)
                    for c in range(4):
                        n = q * 4 + c
                        kb.op(pe, lambda e: e.transpose(out=pf[:, c * 128:(c + 1) * 128],
                                                        in_=big[:, n, b * 128:(b + 1) * 128], identity=identf[:]),
                              reads=["big", "identf"], writes=[pk], sig=(c == 3))
                    kb.op(dve, lambda e: e.tensor_tensor(out=xs[:, q * 512:(q + 1) * 512], in0=pf[:],
                                                         in1=xs[:, q * 512:(q + 1) * 512], op=ALU.add),
                          reads=[pk, xk], writes=[xk])
                kb.dma(sp, dst[b * 128:(b + 1) * 128, :], xs[:], reads=[xk], writes=[dkey_fn(b)], semkey=("xst", b % 2))

        def copy_to_big(ci, ps, pk):
            kb.op(act, lambda e: e.copy(out=big[:, ci, :], in_=ps[:]), reads=[pk], writes=["big"])

        for li in layers:
            j = li // 2
            first = (li == layers[0])
            last = (li == layers[-1])
            kb.barrier()
            kb.dma(sp, vec[:], vecs[li], writes=["vec"])

            def xkey(t):
                return lambda b: ("xr", t, b)

            with ExitStack() as ms:
                if li % 2 == 0:
                    even_mixer(nc, kb, ms, sbt, locals())
                else:
                    odd_mixer(nc, kb, ms, sbt, locals())
            kb.barrier()
            if os.environ.get("KDEBUG_SKIP_MLP"):
                for t in range(NT):
                    for bb in range(NB):
                        kb.dma(sp, y_out[t * TT_ + bb * 128: t * TT_ + (bb + 1) * 128, :],
                               xr[t * TT_ + bb * 128: t * TT_ + (bb + 1) * 128, :], reads=[("xr", t, bb)], writes=[("y", t, bb)])
                continue
            with ExitStack() as ms:
                hid = sbt(ms, "hid", [128, FC, TT_], BF16)
                rl = [sbt(ms, f"rl{i}", [128, TT_], F32) for i in range(2)]
                for t in range(NT):
                    src = xr[t * TT_:(t + 1) * TT_, :]
                    dst = (y_out if last else xr)[t * TT_:(t + 1) * TT_, :]
                    prenorm_transpose(src, xkey(t), 32)

                    def up_cons(ci, ps, pk):
                        r = rl[ci % 2]
                        kb.op(act, lambda e: e.activation(out=r[:], in_=ps[:], func=AF.Relu),
                              reads=[pk], writes=[("rl", ci % 2)])
                        kb.op(dve, lambda e: e.tensor_tensor(out=hid[:, ci, :], in0=r[:], in1=r[:], op=ALU.mult),
                              reads=[("rl", ci % 2)], writes=["hid"])
                    proj(b_wup[li], 0, FC, hT, "hT", ("w", "wup", li), up_cons)
                    for n in range(KC):
                        view, skey = load_slab(b_wdn[li][n].rearrange("p (kc c) -> p kc c", c=128), (FC, 128),
                                               ("w", "wdn", li))
                        ps, pk = next_pf()
                        for kc in range(FC):
                            kb.op(pe, lambda e: e.matmul(ps[:], lhsT=view[:, kc, :], rhs=hid[:, kc, :],
                                                         start=(kc == 0), stop=(kc == FC - 1)),
                                  reads=[skey, "hid"], writes=[pk], sig=(kc == FC - 1))
                        copy_to_big(n, ps, pk)
                    postnorm_residual(48, src, xkey(t), dst, (lambda b, t=t: ("y", t, b)) if last else xkey(t))
            if not last:
                for e in kb.engs:
                    if e.count > 20000:
                        e.new_sem()
        kb.barrier([sp])
        print("built: inst", kb.n_inst, "sems", kb.n_sem, {e.name: e.count for e in kb.engs})
    return nc


def even_mixer(nc, kb, ms, sbt, L):
    pe, act, dve, pool, sp = kb.pe, kb.act, kb.dve, kb.pool, kb.sp
    li, j, NT, first = L["li"], L["j"], L["NT"], L["first"]
    vec, hT, big, junk, ws, PB, PF, PW = L["vec"], L["hT"], L["big"], L["junk"], L["ws"], L["PB"], L["PF"], L["PW"]
    identf, identb, onesf = L["identf"], L["identb"], L["onesf"]
    Um, Im, MC, MS = L["Um"], L["Im"], L["MC"], L["MS"]
    proj, prenorm_transpose, postnorm_residual = L["proj"], L["prenorm_transpose"], L["postnorm_residual"]
    copy_to_big, rsqrt_small, xkey, next_pf = L["copy_to_big"], L["rsqrt_small"], L["xkey"], L["next_pf"]
    x_in, xr, bcs, c_invcnt = L["x_in"], L["xr"], L["bcs"], L["c_invcnt"]
    wb = L["b_ewin"][j]
    wkey = ("w", "ewin", j)
    sq, rb = L["sq"], L["rb"]

    bigb = big[:].rearrange("p a b -> p (a b)").bitcast(BF16)
    qkT = bigb[:, 0:8192].rearrange("p (h w t) -> p h w t", h=NH, w=2)
    vT = bigb[:, 8192:12288].rearrange("p (h t) -> p h t", h=NH)
    zs = bigb[:, 12288:16384].rearrange("p (h t) -> p h t", h=NH)

    bc = sbt(ms, "bc", [128, 144], F32)
    eal = sbt(ms, "eal", [128, 8], F32)
    invc = sbt(ms, "invc", [128, 4, TT_], F32)
    wba = sbt(ms, "wba", [128, KC, 16], BF16)
    pw = sbt(ms, "pw", [128, 8, 256], BF16)
    hq = sbt(ms, "hq", [128, 24, 3], F32)
    hxp = sbt(ms, "hxp", [128, 8, 15], F32)
    S = sbt(ms, "S", [128, NH, 128], F32)
    Sb = sbt(ms, "Sb", [128, NH, 128], BF16)
    pbuf = sbt(ms, "pbuf", [128, TT_ + 3], F32)
    acc = sbt(ms, "acc", [128, TT_], F32)
    sfl = sbt(ms, "sfl", [128, TT_], F32)
    pe_ = sbt(ms, "pe_", [128, TT_ + 15], F32)
    ta = sbt(ms, "ta", [128, TT_ + 15], F32)
    tb = sbt(ms, "tb", [128, TT_ + 15], F32)
    pl = sbt(ms, "pl", [128, 8, TT_], BF16)
    lg = sbt(ms, "lg", [64, NCH, 16], F32)
    F = [sbt(ms, f"F{i}", [64, NH, 64], F32) for i in range(8)]
    AT = sbt(ms, "AT", [64, NH, 64], BF16)
    TTb = sbt(ms, "TTb", [64, NH, 64], BF16)
    vb = sbt(ms, "vb", [64, NH, 128], BF16)
    kbG = sbt(ms, "kbG", [64, NH, 128], BF16)
    kd = sbt(ms, "kd", [64, NH, 128], BF16)
    vn = sbt(ms, "vn", [64, NH, 128], BF16)
    on = sbt(ms, "on", [64, NH, 128], BF16)
    nwT = sbt(ms, "nwT", [128, NH, 64], BF16)
    qgT = sbt(ms, "qgT", [128, NH, 64], BF16)
    osq = sbt(ms, "osq", [64, NH, 128], F32)
    sm = sbt(ms, "sm", [128, 16, 8], F32)

    kb.dma(sp, bc[:], bcs[j], writes=["bc"])
    kb.dma(sp, invc[:], c_invcnt.rearrange("p (a t) -> p a t", a=4), writes=["invc"])
    kb.dma(sp, wba[:], wb[:, 4096:4112].rearrange("(kc p) n -> p kc n", p=128), reads=[wkey], writes=["wba"])
    kb.dma(sp, pw[:], L["b_epw"][j].rearrange("(g p) d -> p g d", p=128), reads=[("w", "epw", j)], writes=["pw"])
    kb.op(act, lambda e: e.activation(out=eal[:], in_=bc[:, 0:8], func=AF.Exp), reads=["bc"], writes=["eal"])
    for buf, key in ((hq, "hq"), (hxp, "hxp"), (S, "S")):
        kb.op(dve, lambda e: e.memset(buf[:], 0.0), writes=[key])
    kb.op(dve, lambda e: e.memset(Sb[:], 0.0), writes=["Sb"])
    cw = vec[:, 64:160].rearrange("p (c j) -> p c j", j=4)
    pscale = vec[:, 160:168]
    dtb = bc[:, 8:16]
    dng = bc[:, 16:144]

    def bcast_h(ap2, n):
        return ap2.unsqueeze(2).to_broadcast([ap2.shape[0], NH, n])

    def bcast_m(ap2):
        return ap2.unsqueeze(1).to_broadcast([64, NH, 64])

    for t in range(NT):
        src = (x_in if first else xr)[t * TT_:(t + 1) * TT_, :]
        skey = (lambda b: ("xin",)) if first else xkey(t)
        dst = xr[t * TT_:(t + 1) * TT_, :]
        prenorm_transpose(src, skey, 0)

        def qkv_cons(ci, ps, pk):
            kb.op(act, lambda e: e.copy(out=pbuf[:, 0:3], in_=hq[:, ci, :]), reads=["hq"], writes=["pbuf"])
            kb.op(act, lambda e: e.copy(out=pbuf[:, 3:TT_ + 3], in_=ps[:]), reads=[pk], writes=["pbuf"])
            kb.op(act, lambda e: e.copy(out=hq[:, ci, :], in_=pbuf[:, TT_:TT_ + 3]), reads=["pbuf"], writes=["hq"])
            kb.op(dve, lambda e: e.tensor_scalar_mul(out=acc[:], in0=pbuf[:, 0:TT_], scalar1=cw[:, ci, 0:1]),
                  reads=["pbuf", "vec"], writes=["acc"])
            for jj in range(1, 4):
                kb.op(dve, lambda e: e.scalar_tensor_tensor(out=acc[:], in0=pbuf[:, jj:jj + TT_],
                                                            scalar=cw[:, ci, jj:jj + 1], in1=acc[:],
                                                            op0=ALU.mult, op1=ALU.add),
                      reads=["pbuf", "vec", "acc"], writes=["acc"])
            which, h = ci // 8, ci % 8
            if which == 2:
                kb.op(act, lambda e: e.activation(out=vT[:, h, :], in_=acc[:], func=AF.Silu),
                      reads=["acc"], writes=["big"])
                return
            kb.op(act, lambda e: e.activation(out=sfl[:], in_=acc[:], func=AF.Silu), reads=["acc"], writes=["sfl"])
            s = sq[ci % 2]
            kb.op(act, lambda e: e.activation(out=s[:], in_=sfl[:], func=AF.Square),
                  reads=["sfl"], writes=[("sq", ci % 2)])
            kb.op(pe, lambda e: e.matmul(PF[2][:], lhsT=onesf[:], rhs=s[:], start=True, stop=True),
                  reads=[("sq", ci % 2), "onesf"], writes=[("PF", 2)])
            rsqrt_small(rb[:], PF[2][:], 1.0, [("PF", 2)], ["rb"])
            dstv = qkT[:, h, 1 if which == 0 else 0, :]
            kb.op(dve, lambda e: e.scalar_tensor_tensor(out=dstv, in0=sfl[:], scalar=(128.0 ** -0.5 if which == 0 else 1.0),
                                                        in1=rb[:], op0=ALU.mult, op1=ALU.mult),
                  reads=["sfl", "rb"], writes=["big"])
        proj(wb, 0, 24, hT, "hT", wkey, qkv_cons)

        def z_cons(h, ps, pk):
            kb.op(act, lambda e: e.activation(out=zs[:, h, :], in_=ps[:], func=AF.Silu), reads=[pk], writes=["big"])
        proj(wb, 3072, 8, hT, "hT", wkey, z_cons)

        def xp_cons(c, ps, pk):
            gi = c // 2
            kb.op(act, lambda e: e.copy(out=pe_[:, 0:15], in_=hxp[:, c, :]), reads=["hxp"], writes=["pe_"])
            kb.op(act, lambda e: e.copy(out=pe_[:, 15:TT_ + 15], in_=ps[:]), reads=[pk], writes=["pe_"])
            kb.op(act, lambda e: e.copy(out=hxp[:, c, :], in_=pe_[:, TT_:TT_ + 15]), reads=["pe_"], writes=["hxp"])
            W = TT_ + 15
            kb.op(dve, lambda e: e.tensor_tensor(out=ta[:, 1:W], in0=pe_[:, 1:W], in1=pe_[:, 0:W - 1], op=ALU.add),
                  reads=["pe_"], writes=["ta"])
            r, rk = ta, "ta"
            if gi >= 1:
                kb.op(dve, lambda e: e.tensor_tensor(out=tb[:, 3:W], in0=ta[:, 3:W], in1=ta[:, 1:W - 2], op=ALU.add),
                      reads=["ta"], writes=["tb"])
                r, rk = tb, "tb"
            if gi >= 2:
                kb.op(dve, lambda e: e.tensor_tensor(out=ta[:, 7:W], in0=tb[:, 7:W], in1=tb[:, 3:W - 4], op=ALU.add),
                      reads=["tb"], writes=["ta"])
                r, rk = ta, "ta"
            if gi >= 3:
                kb.op(dve, lambda e: e.tensor_tensor(out=tb[:, 15:W], in0=ta[:, 15:W], in1=ta[:, 7:W - 8], op=ALU.add),
                      reads=["ta"], writes=["tb"])
                r, rk = tb, "tb"
            if t == 0:
                kb.op(dve, lambda e: e.tensor_tensor(out=acc[:], in0=r[:, 15:W], in1=invc[:, gi, :], op=ALU.mult),
                      reads=[rk, "invc"], writes=["acc"])
                kb.op(dve, lambda e: e.tensor_tensor(out=pl[:, c, :], in0=acc[:], in1=pe_[:, 15:W], op=ALU.subtract),
                      reads=["acc", "pe_"], writes=["pl"])
            else:
                kb.op(dve, lambda e: e.scalar_tensor_tensor(out=pl[:, c, :], in0=r[:, 15:W], scalar=1.0 / (2 << gi),
                                                            in1=pe_[:, 15:W], op0=ALU.mult, op1=ALU.subtract),
                      reads=[rk, "pe_"], writes=["pl"])
        proj(wb, 4112, 8, hT, "hT", wkey, xp_cons)

        for ck in range(NCH):
            for kc in range(KC):
                kb.op(pe, lambda e: e.matmul(PF[2][0:64, 0:16], lhsT=hT[:, kc, ck * CH:(ck + 1) * CH], rhs=wba[:, kc, :],
                                             start=(kc == 0), stop=(kc == KC - 1)),
                      reads=["hT", "wba"], writes=[("PF", 2)], sig=(kc == KC - 1))
            kb.op(act, lambda e: e.copy(out=lg[:, ck, :], in_=PF[2][0:64, 0:16]), reads=[("PF", 2)], writes=["lg"])

        mixT = hT
        for gi in range(4):
            for dc in range(2):
                ps, pk = next_pf()
                for cc in range(2):
                    kb.op(pe, lambda e: e.matmul(ps[:], lhsT=pw[:, gi * 2 + cc, dc * 128:(dc + 1) * 128],
                                                 rhs=pl[:, gi * 2 + cc, :], start=(cc == 0), stop=(cc == 1)),
                          reads=["pw", "pl"], writes=[pk], sig=(cc == 1))
                col = gi * 2 + dc
                kb.op(dve, lambda e: e.tensor_scalar_mul(out=mixT[:, 8 + col, :], in0=ps[:],
                                                         scalar1=pscale[:, col:col + 1]),
                      reads=[pk, "vec"], writes=["hT"])

        for ck in range(NCH):
            t0 = ck * CH
            sl = slice(t0, t0 + CH)
            s_ = lambda i: sm[0:64, i, :]
            E1, S1, BETA, LNB, AXX, E2, SPL, G, GCS, GEX, KDS, BG, OSS, ORS = (s_(i) for i in range(14))
            GL = sm[:, 14, :]
            DEC = sm[:, 15, :]
            R = lambda *a, **k: None
            kb.op(act, lambda e: e.activation(out=E1, in_=lg[:, ck, 0:8], func=AF.Exp, scale=-1.0),
                  reads=["lg"], writes=["sm0"])
            kb.op(dve, lambda e: e.tensor_scalar_add(out=S1, in0=E1, scalar1=1.0),
                  reads=["sm0"], writes=["sm1"])
            kb.op(dve, lambda e: e.reciprocal(out=BETA, in_=S1), reads=["sm1"], writes=["sm2"])
            kb.op(act, lambda e: e.activation(out=LNB, in_=S1, func=AF.Ln), reads=["sm1"], writes=["sm3"])
            kb.op(dve, lambda e: e.tensor_tensor(out=AXX, in0=lg[:, ck, 8:16], in1=dtb[0:64, :], op=ALU.add),
                  reads=["lg", "bc"], writes=["sm4"])
            kb.op(act, lambda e: e.activation(out=E2, in_=AXX, func=AF.Exp), reads=["sm4"], writes=["sm5"])
            kb.op(act, lambda e: e.activation(out=SPL, in_=E2, func=AF.Ln, bias=1.0), reads=["sm5"], writes=["sm6"])
            kb.op(dve, lambda e: e.scalar_tensor_tensor(out=G, in0=SPL, scalar=-1.0, in1=eal[0:64, :],
                                                        op0=ALU.mult, op1=ALU.mult),
                  reads=["sm6", "eal"], writes=["sm7"])
            kb.op(pe, lambda e: e.matmul(PF[2][0:64, 32:40], lhsT=Um, rhs=G, start=True, stop=True),
                  reads=["masks", "sm7"], writes=[("PF", 2)])
            kb.op(pe, lambda e: e.matmul(PF[3][:, 0:8], lhsT=onesf[0:64, :], rhs=G, start=True, stop=True),
                  reads=["onesf", "sm7"], writes=[("PF", 3)])
            kb.op(act, lambda e: e.copy(out=GCS, in_=PF[2][0:64, 32:40]), reads=[("PF", 2)], writes=["sm8"])
            kb.op(act, lambda e: e.copy(out=GL, in_=PF[3][:, 0:8]), reads=[("PF", 3)], writes=["sm14"])
            kb.op(act, lambda e: e.activation(out=GEX, in_=GCS, func=AF.Exp), reads=["sm8"], writes=["sm9"])
            kb.op(dve, lambda e: e.tensor_tensor(out=KDS, in0=GL[0:64, :], in1=GCS, op=ALU.subtract),
                  reads=["sm14", "sm8"], writes=["sm10"])
            kb.op(act, lambda e: e.activation(out=KDS, in_=KDS, func=AF.Exp), reads=["sm10"], writes=["sm10"])
            kb.op(act, lambda e: e.activation(out=DEC, in_=GL, func=AF.Exp), reads=["sm14"], writes=["sm15"])
            kb.op(dve, lambda e: e.tensor_tensor(out=BG, in0=BETA, in1=GEX, op=ALU.mult),
                  reads=["sm2", "sm9"], writes=["sm11"])
            R1, R1p, R2, T1, GT, GbT, NTm, Nm = F
            kb.op(dve, lambda e: e.tensor_tensor(out=R1[:], in0=bcast_h(G, 64), in1=bcast_m(Um), op=ALU.mult),
                  reads=["sm7", "masks"], writes=["F0"])
            kb.op(dve, lambda e: e.tensor_tensor(out=T1[:], in0=bcast_h(LNB, 64), in1=bcast_m(Im), op=ALU.mult),
                  reads=["sm3", "masks"], writes=["F3"])
            kb.op(dve, lambda e: e.tensor_tensor(out=R1p[:], in0=R1[:], in1=T1[:], op=ALU.subtract),
                  reads=["F0", "F3"], writes=["F1"])
            kb.op(dve, lambda e: e.tensor_scalar_mul(out=R2[:], in0=bcast_h(G, 64), scalar1=-1.0),
                  reads=["sm7"], writes=["F2"])
            fl = lambda a: a[:].rearrange("p h c -> p (h c)")
            kb.op(pe, lambda e: e.matmul(PF[2][0:64, :], lhsT=onesf[0:64, 0:64], rhs=fl(R1), start=True, stop=False),
                  reads=["onesf", "F0"], writes=[("PF", 2)], sig=False)
            kb.op(pe, lambda e: e.matmul(PF[2][0:64, :], lhsT=Um, rhs=fl(R2), start=False, stop=True),
                  reads=["masks", "F2"], writes=[("PF", 2)])
            kb.op(pe, lambda e: e.matmul(PF[3][0:64, :], lhsT=onesf[0:64, 0:64], rhs=fl(R1p), start=True, stop=False),
                  reads=["onesf", "F1"], writes=[("PF", 3)], sig=False)
            kb.op(pe, lambda e: e.matmul(PF[3][0:64, :], lhsT=Um, rhs=fl(R2), start=False, stop=True),
                  reads=["masks", "F2"], writes=[("PF", 3)])
            v3 = lambda ps: ps[0:64, :].rearrange("p (h c) -> p h c", h=NH)
            kb.op(dve, lambda e: e.tensor_tensor(out=T1[:], in0=v3(PF[2]), in1=bcast_m(MC), op=ALU.add),
                  reads=[("PF", 2), "masks"], writes=["F3"])
            kb.op(act, lambda e: e.activation(out=GT[:], in_=T1[:], func=AF.Exp), reads=["F3"], writes=["F4"])
            kb.op(dve, lambda e: e.tensor_tensor(out=T1[:], in0=v3(PF[3]), in1=bcast_m(MS), op=ALU.add),
                  reads=[("PF", 3), "masks"], writes=["F3"])
            kb.op(act, lambda e: e.activation(out=GbT[:], in_=T1[:], func=AF.Exp), reads=["F3"], writes=["F5"])
            for h in range(NH):
                kb.op(pe, lambda e: e.matmul(PW[0:64, h * 128:(h + 1) * 128], lhsT=qkT[:, h, 0, sl],
                                             rhs=qkT[:, h, :, sl], start=True, stop=True),
                      reads=["big"], writes=["PW"], sig=(h == NH - 1))
            pw4 = PW[0:64, :].rearrange("p (h w c) -> p h w c", h=NH, w=2)
            kb.op(dve, lambda e: e.scalar_tensor_tensor(out=NTm[:], in0=pw4[:, :, 0, :], scalar=-1.0, in1=GbT[:],
                                                        op0=ALU.mult, op1=ALU.mult),
                  reads=["PW", "F5"], writes=["F6"])
            kb.op(dve, lambda e: e.tensor_tensor(out=AT[:], in0=pw4[:, :, 1, :], in1=GT[:], op=ALU.mult),
                  reads=["PW", "F4"], writes=["AT"])
            for h in range(NH):
                kb.op(pe, lambda e: e.transpose(out=PF[2][0:64, h * 64:(h + 1) * 64], in_=NTm[:, h, :],
                                                identity=identf[0:64, 0:64]),
                      reads=["F6", "identf"], writes=[("PF", 2)], sig=(h == NH - 1))
            kb.op(act, lambda e: e.copy(out=Nm[:], in_=v3(PF[2])), reads=[("PF", 2)], writes=["F7"])
            TTm = F[0]
            kb.op(dve, lambda e: e.tensor_tensor(out=TTm[:], in0=NTm[:], in1=bcast_m(Im), op=ALU.add),
                  reads=["F6", "masks"], writes=["F0"])
            Pc, PTc, Pck, PTck = Nm, NTm, "F7", "F6"
            free = [(F[1], "F1"), (F[2], "F2"), (F[3], "F3"), (F[4], "F4")]
            for lvl in range(1, 6):
                Pn, Pnk = free.pop(0)
                for h in range(NH):
                    kb.op(pe, lambda e: e.matmul(PF[2][0:64, h * 64:(h + 1) * 64], lhsT=PTc[:, h, :], rhs=Pc[:, h, :],
                                                 start=True, stop=True),
                          reads=[Pck, PTck], writes=[("PF", 2)], sig=(h == NH - 1))
                kb.op(act, lambda e: e.copy(out=Pn[:], in_=v3(PF[2])), reads=[("PF", 2)], writes=[Pnk])
                if lvl < 5:
                    PTn, PTnk = free.pop(0)
                    for h in range(NH):
                        kb.op(pe, lambda e: e.matmul(PF[3][0:64, h * 64:(h + 1) * 64], lhsT=Pc[:, h, :], rhs=PTc[:, h, :],
                                                     start=True, stop=True),
                              reads=[Pck, PTck], writes=[("PF", 3)], sig=(h == NH - 1))
                    kb.op(act, lambda e: e.copy(out=PTn[:], in_=v3(PF[3])), reads=[("PF", 3)], writes=[PTnk])
                for h in range(NH):
                    kb.op(pe, lambda e: e.matmul(PF[0][0:64, h * 64:(h + 1) * 64], lhsT=Pn[:, h, :], rhs=TTm[:, h, :],
                                                 start=True, stop=True),
                          reads=[Pnk, "F0"], writes=[("PF", 0)], sig=(h == NH - 1))
                kb.op(dve, lambda e: e.tensor_tensor(out=TTm[:], in0=TTm[:], in1=v3(PF[0]), op=ALU.add),
                      reads=[("PF", 0), "F0"], writes=["F0"])
                if lvl < 5:
                    free.append((Pc, Pck))
                    free.append((PTc, PTck))
                    Pc, PTc, Pck, PTck = Pn, PTn, Pnk, PTnk
            kb.op(act, lambda e: e.copy(out=TTb[:], in_=TTm[:]), reads=["F0"], writes=["TTb"])
            for h in range(NH):
                kb.op(pe, lambda e: e.transpose(out=PB[0][0:64, h * 128:(h + 1) * 128], in_=qkT[:, h, 0, sl],
                                                identity=identb[:]),
                      reads=["big", "identb"], writes=[("PB", 0)], sig=(h == NH - 1))
            for h in range(NH):
                kb.op(pe, lambda e: e.transpose(out=PB[1][0:64, h * 128:(h + 1) * 128], in_=vT[:, h, sl],
                                                identity=identb[:]),
                      reads=["big", "identb"], writes=[("PB", 1)], sig=(h == NH - 1))
            k3 = PB[0][0:64, :].rearrange("p (h d) -> p h d", h=NH)
            v3b = PB[1][0:64, :].rearrange("p (h d) -> p h d", h=NH)
            kb.op(dve, lambda e: e.tensor_tensor(out=vb[:], in0=v3b, in1=bcast_h(BETA, 128), op=ALU.mult),
                  reads=[("PB", 1), "sm2"], writes=["vb"])
            kb.op(dve, lambda e: e.tensor_tensor(out=kbG[:], in0=k3, in1=bcast_h(BG, 128), op=ALU.mult),
                  reads=[("PB", 0), "sm11"], writes=["kbG"])
            kb.op(dve, lambda e: e.tensor_tensor(out=kd[:], in0=k3, in1=bcast_h(KDS, 128), op=ALU.mult),
                  reads=[("PB", 0), "sm10"], writes=["kd"])
            for h in range(NH):
                kb.op(pe, lambda e: e.matmul(PF[0][:, h * 64:(h + 1) * 64], lhsT=kbG[:, h, :], rhs=TTb[:, h, :],
                                             start=True, stop=True),
                      reads=["kbG", "TTb"], writes=[("PF", 0)], sig=(h == NH - 1))
            kb.op(dve, lambda e: e.tensor_scalar_mul(out=nwT[:], in0=PF[0][:].rearrange("p (h c) -> p h c", h=NH),
                                                     scalar1=-1.0),
                  reads=[("PF", 0)], writes=["nwT"])
            Gd = F[1]
            kb.op(dve, lambda e: e.tensor_tensor(out=Gd[:], in0=bcast_h(GEX, 64), in1=bcast_m(Im), op=ALU.mult),
                  reads=["sm9", "masks"], writes=["F1"])
            kb.op(pe, lambda e: e.matmul(PF[1][:], lhsT=onesf[0:64, :], rhs=fl(Gd), start=True, stop=True),
                  reads=["onesf", "F1"], writes=[("PF", 1)])
            kb.op(dve, lambda e: e.tensor_tensor(out=qgT[:], in0=qkT[:, :, 1, sl],
                                                 in1=PF[1][:].rearrange("p (h c) -> p h c", h=NH), op=ALU.mult),
                  reads=["big", ("PF", 1)], writes=["qgT"])
            for h in range(NH):
                kb.op(pe, lambda e: e.matmul(PW[0:64, h * 128:(h + 1) * 128], lhsT=TTb[:, h, :], rhs=vb[:, h, :],
                                             start=True, stop=False),
                      reads=["TTb", "vb"], writes=["PW"], sig=False)
                kb.op(pe, lambda e: e.matmul(PW[0:64, h * 128:(h + 1) * 128], lhsT=nwT[:, h, :], rhs=Sb[:, h, :],
                                             start=False, stop=True),
                      reads=["nwT", "Sb"], writes=["PW"], sig=(h == NH - 1))
            kb.op(act, lambda e: e.copy(out=vn[:], in_=PW[0:64, :].rearrange("p (h d) -> p h d", h=NH)),
                  reads=["PW"], writes=["vn"])
            for h in range(NH):
                po = PF[2 + h // 4]
                pk = ("PF", 2 + h // 4)
                hh = h % 4
                kb.op(pe, lambda e: e.matmul(po[0:64, hh * 128:(hh + 1) * 128], lhsT=qgT[:, h, :], rhs=Sb[:, h, :],
                                             start=True, stop=False),
                      reads=["qgT", "Sb"], writes=[pk], sig=False)
                kb.op(pe, lambda e: e.matmul(po[0:64, hh * 128:(hh + 1) * 128], lhsT=AT[:, h, :], rhs=vn[:, h, :],
                                             start=False, stop=True),
                      reads=["AT", "vn"], writes=[pk], sig=(hh == 3))
            for h in range(NH):
                kb.op(pe, lambda e: e.matmul(PW[:, h * 128:(h + 1) * 128], lhsT=kd[:, h, :], rhs=vn[:, h, :],
                                             start=True, stop=True),
                      reads=["kd", "vn"], writes=["PW"], sig=(h == NH - 1))
            kb.op(dve, lambda e: e.tensor_tensor(out=S[:], in0=S[:], in1=bcast_h(DEC, 128), op=ALU.mult),
                  reads=["S", "sm15"], writes=["S"])
            kb.op(dve, lambda e: e.tensor_tensor(out=S[:], in0=S[:], in1=PW[:].rearrange("p (h d) -> p h d", h=NH),
                                                 op=ALU.add),
                  reads=["S", "PW"], writes=["S"])
            kb.op(act, lambda e: e.copy(out=Sb[:], in_=S[:]), reads=["S"], writes=["Sb"])
            dump = L["dump"]
            if dump is not None and t == 0 and ck == int(os.environ.get("KDEBUG_CK", "0")):
                dump("lg", lg[:, ck, :], "lg")
                dump("G", G, "sm7"); dump("BETA", BETA, "sm2"); dump("GCS", GCS, "sm8"); dump("KDS", KDS, "sm10")
                dump("DEC", DEC, "sm15")
                dump("q", qkT[:, :, 1, sl], "big"); dump("k", qkT[:, :, 0, sl], "big"); dump("v", vT[:, :, sl], "big")
                dump("AT", AT[:], "AT"); dump("TT", TTm[:], "F0")
                dump("vn", vn[:], "vn"); dump("kd", kd[:], "kd"); dump("qgT", qgT[:], "qgT"); dump("nwT", nwT[:], "nwT")
                dump("S", S[:, 0:4, :], "S")
            for half in range(2):
                po = PF[2 + half]
                pk = ("PF", 2 + half)
                o3 = po[0:64, :].rearrange("p (h d) -> p h d", h=4)
                hs = slice(half * 4, half * 4 + 4)
                kb.op(act, lambda e: e.activation(out=osq[:, hs, :], in_=o3, func=AF.Square), reads=[pk], writes=["osq"])
                kb.op(dve, lambda e: e.tensor_reduce(out=OSS[:, hs], in_=osq[:, hs, :], axis=AX.X, op=ALU.add),
                      reads=["osq"], writes=["sm12"])
                kb.op(act, lambda e: e.activation(out=ORS[:, hs], in_=OSS[:, hs], func=AF.Sqrt, scale=1.0 / 128, bias=EPS),
                      reads=["sm12"], writes=["sm13"])
                kb.op(dve, lambda e: e.reciprocal(out=ORS[:, hs], in_=ORS[:, hs]), reads=["sm13"], writes=["sm13"])
                kb.op(dve, lambda e: e.tensor_tensor(out=osq[:, hs, :], in0=o3,
                                                     in1=ORS[:, hs].unsqueeze(2).to_broadcast([64, 4, 128]), op=ALU.mult),
                      reads=[pk, "sm13"], writes=["osq"])
                kb.op(dve, lambda e: e.tensor_tensor(out=on[:, hs, :], in0=osq[:, hs, :],
                                                     in1=dng[0:64, :].unsqueeze(1).to_broadcast([64, 4, 128]),
                                                     op=ALU.mult),
                      reads=["osq", "bc"], writes=["on"])
            if dump is not None and t == 0 and ck == int(os.environ.get("KDEBUG_CK", "0")):
                dump("on", on[:], "on")
            for h in range(NH):
                kb.op(pe, lambda e: e.transpose(out=PB[0][:, h * 64:(h + 1) * 64], in_=on[:, h, :],
                                                identity=identb[0:64, 0:64]),
                      reads=["on", "identb"], writes=[("PB", 0)], sig=(h == NH - 1))
            kb.op(dve, lambda e: e.tensor_tensor(out=mixT[:, 0:NH, sl],
                                                 in0=PB[0][:, 0:512].rearrange("p (h c) -> p h c", h=NH),
                                                 in1=zs[:, :, sl], op=ALU.mult),
                  reads=[("PB", 0), "big"], writes=["hT"])

        proj(L["b_ewout"][j], 0, KC, mixT, "hT", ("w", "ewout", j), copy_to_big)
        postnorm_residual(16, src, skey, dst, xkey(t))


def odd_mixer(nc, kb, ms, sbt, L):
    pe, act, dve, pool, sp = kb.pe, kb.act, kb.dve, kb.pool, kb.sp
    li, j, NT, first = L["li"], L["j"], L["NT"], L["first"]
    vec, hT, big, PF = L["vec"], L["hT"], L["big"], L["PF"]
    onesf, sq, rb = L["onesf"], L["sq"], L["rb"]
    proj, prenorm_transpose, postnorm_residual = L["proj"], L["prenorm_transpose"], L["postnorm_residual"]
    copy_to_big, xkey, next_pf, load_slab = L["copy_to_big"], L["xkey"], L["next_pf"], L["load_slab"]
    x_in, xr = L["x_in"], L["xr"]
    wb = L["b_owin"][j]
    wkey = ("w", "owin", j)
    WIN = 31
    ub = sbt(ms, "ub", [128, KC, TT_ + 30], F32)
    sg = sbt(ms, "sg", [128, TT_], F32)
    mean = sbt(ms, "mean", [128, TT_], F32)
    var = sbt(ms, "var", [128, TT_], F32)
    dw = vec[:, 64:560].rearrange("p (c j) -> p c j", j=WIN)
    dwb, lng, lnb = vec[:, 560:576], vec[:, 576:592], vec[:, 592:608]
    kb.op(dve, lambda e: e.memset(ub[:], 0.0), writes=["ub"])
    for t in range(NT):
        src = (x_in if first else xr)[t * TT_:(t + 1) * TT_, :]
        skey = (lambda b: ("xin",)) if first else xkey(t)
        dst = xr[t * TT_:(t + 1) * TT_, :]
        prenorm_transpose(src, skey, 0)
        for g0 in range(0, KC, 2):
            s = L["ws_ctr"][0] % 2
            L["ws_ctr"][0] += 1
            view = L["ws"][s][:, 0:KC * 512].rearrange("p (a b) -> p a b", b=512)
            for part in range(2):
                srcw = wb[:, part * D + g0 * 128: part * D + (g0 + 2) * 128].rearrange("(kc p) n -> p kc n", p=128)
                kb.dma(sp, view[:, :, part * 256:(part + 1) * 256], srcw, reads=[wkey], writes=[("ws", s)])
            for c in range(2):
                ci = g0 + c
                pa, pg = PF[0], PF[1]
                for part, ps, pk in ((0, pa, ("PF", 0)), (1, pg, ("PF", 1))):
                    for kc in range(KC):
                        kb.op(pe, lambda e: e.matmul(ps[:], lhsT=view[:, kc, part * 256 + c * 128: part * 256 + (c + 1) * 128],
                                                     rhs=hT[:, kc, :], start=(kc == 0), stop=(kc == KC - 1)),
                              reads=[("ws", s), "hT"], writes=[pk], sig=(kc == KC - 1))
                kb.op(act, lambda e: e.activation(out=sg[:], in_=pg[:], func=AF.Sigmoid), reads=[("PF", 1)], writes=["sg"])
                kb.op(dve, lambda e: e.tensor_tensor(out=ub[:, ci, 30:TT_ + 30], in0=pa[:], in1=sg[:], op=ALU.mult),
                      reads=[("PF", 0), "sg", "ub"], writes=["ub"])
        for c in range(KC):
            kb.op(dve, lambda e: e.tensor_scalar(out=big[:, c, :], in0=ub[:, c, 0:TT_], scalar1=dw[:, c, 0:1],
                                                 scalar2=dwb[:, c:c + 1], op0=ALU.mult, op1=ALU.add),
                  reads=["ub", "vec"], writes=["big"])
            for jj in range(1, WIN):
                kb.op(dve, lambda e: e.scalar_tensor_tensor(out=big[:, c, :], in0=ub[:, c, jj:jj + TT_],
                                                            scalar=dw[:, c, jj:jj + 1], in1=big[:, c, :],
                                                            op0=ALU.mult, op1=ALU.add),
                      reads=["ub", "vec", "big"], writes=["big"])
            s = sq[c % 2]
            kb.op(act, lambda e: e.activation(out=s[:], in_=big[:, c, :], func=AF.Square),
                  reads=["big"], writes=[("sq", c % 2)])
            kb.op(pe, lambda e: e.matmul(PF[2][:], lhsT=onesf[:], rhs=big[:, c, :], start=(c == 0), stop=(c == KC - 1)),
                  reads=["big", "onesf"], writes=[("PF", 2)])
            kb.op(pe, lambda e: e.matmul(PF[3][:], lhsT=onesf[:], rhs=s[:], start=(c == 0), stop=(c == KC - 1)),
                  reads=[("sq", c % 2), "onesf"], writes=[("PF", 3)])
        kb.op(act, lambda e: e.copy(out=ub[:, :, 0:30], in_=ub[:, :, TT_:TT_ + 30]), reads=["ub", "big"], writes=["ub"])
        kb.op(dve, lambda e: e.tensor_scalar_mul(out=mean[:], in0=PF[2][:], scalar1=1.0 / D),
              reads=[("PF", 2)], writes=["mean"])
        kb.op(dve, lambda e: e.tensor_tensor(out=var[:], in0=mean[:], in1=mean[:], op=ALU.mult),
              reads=["mean"], writes=["var"])
        kb.op(dve, lambda e: e.scalar_tensor_tensor(out=var[:], in0=PF[3][:], scalar=1.0 / D, in1=var[:],
                                                    op0=ALU.mult, op1=ALU.subtract),
              reads=[("PF", 3), "var"], writes=["var"])
        kb.op(act, lambda e: e.activation(out=var[:], in_=var[:], func=AF.Sqrt, bias=EPS), reads=["var"], writes=["var"])
        kb.op(dve, lambda e: e.reciprocal(out=var[:], in_=var[:]), reads=["var"], writes=["var"])
        for c in range(KC):
            kb.op(dve, lambda e: e.tensor_tensor(out=big[:, c, :], in0=big[:, c, :], in1=mean[:], op=ALU.subtract),
                  reads=["big", "mean"], writes=["big"])
            kb.op(dve, lambda e: e.tensor_tensor(out=big[:, c, :], in0=big[:, c, :], in1=var[:], op=ALU.mult),
                  reads=["big", "var"], writes=["big"])
            kb.op(act, lambda e: e.activation(out=hT[:, c, :], in_=big[:, c, :], func=AF.Silu,
                                              scale=lng[:, c:c + 1], bias=lnb[:, c:c + 1]),
                  reads=["big", "vec"], writes=["hT"])
        proj(L["b_owout"][j], 0, KC, hT, "hT", ("w", "owout", j), copy_to_big)
        postnorm_residual(16, src, skey, dst, xkey(t))


def _fm(v):
    v = np.asarray(v, dtype=np.float32)
    return np.ascontiguousarray(v.reshape(-1, 128).T)


def make_aux(inp):
    vecs = np.zeros((4, 128, NVEC), np.float32)
    bcs = np.zeros((2, 128, 144), np.float32)
    for li in range(4):
        j = li // 2
        vecs[li, :, 0:16] = _fm(inp["norm_mix_pre"][li])
        vecs[li, :, 16:32] = _fm(inp["norm_mix_post"][li])
        vecs[li, :, 32:48] = _fm(inp["norm_mlp_pre"][li])
        vecs[li, :, 48:64] = _fm(inp["norm_mlp_post"][li])
        if li % 2 == 0:
            cw = np.asarray(inp["even_conv"][j], np.float32)
            vecs[li, :, 64:160] = cw.reshape(4, 24, 128).transpose(2, 1, 0).reshape(128, 96)
            vecs[li, :, 160:168] = _fm(inp["even_pool_scale"][j])
            bcs[j, :, 0:8] = np.broadcast_to(np.asarray(inp["even_a_log"][j], np.float32), (128, 8))
            bcs[j, :, 8:16] = np.broadcast_to(np.asarray(inp["even_dt_bias"][j], np.float32), (128, 8))
            bcs[j, :, 16:144] = np.broadcast_to(np.asarray(inp["even_dn_norm"][j], np.float32), (128, 128))
        else:
            dw = np.asarray(inp["odd_dw"][j], np.float32)
            vecs[li, :, 64:560] = dw.reshape(31, 16, 128).transpose(2, 1, 0).reshape(128, 496)
            vecs[li, :, 560:576] = _fm(inp["odd_dw_b"][j])
            vecs[li, :, 576:592] = _fm(inp["odd_ln_g"][j])
            vecs[li, :, 592:608] = _fm(inp["odd_ln_b"][j])
    jj = np.arange(64)
    U = (jj[:, None] <= jj[None, :]).astype(np.float32)
    I = np.eye(64, dtype=np.float32)
    MC = np.where(jj[None, :] >= jj[:, None], 0.0, NEG).astype(np.float32)
    MS = np.where(jj[None, :] > jj[:, None], 0.0, NEG).astype(np.float32)
    masks = np.concatenate([U, I, MC, MS], axis=1)
    tpos = np.arange(TT_) + 1
    invc = np.stack([1.0 / np.minimum(tpos, w) for w in (2, 4, 8, 16)]).astype(np.float32).reshape(1, 4 * TT_)
    return dict(vecs=vecs, bcs=bcs, c_ident=np.eye(128, dtype=np.float32), c_masks=masks,
                c_invcnt=np.ascontiguousarray(np.broadcast_to(invc, (128, 4 * TT_))))


def kernel(**inputs):
    inp = {k: np.asarray(v) for k, v in inputs.items()}
    x = np.ascontiguousarray(inp["x"], dtype=np.float32)
    B = x.shape[0]
    aux = make_aux(inp)
    shared = dict(aux)
    shared["even_w_in"] = np.ascontiguousarray(inp["even_w_in"], dtype=np.float32)
    shared["even_pool_w"] = np.ascontiguousarray(inp["even_pool_w"], dtype=np.float32).reshape(2, 1024, 256)
    for k in ("even_w_out", "odd_w_in", "odd_w_out", "mlp_w_up", "mlp_w_down"):
        shared[k] = np.ascontiguousarray(inp[k], dtype=np.float32)
    nc = build_program(x.shape[1])
    in_maps = [dict(shared, x=x[b]) for b in range(B)]
    res = run_bass_kernel_spmd(nc, in_maps, core_ids=list(range(B)))
    return np.stack([np.asarray(r["y"], dtype=np.float32) for r in res.results], axis=0)
```
